# Optimizing a Trainium2 kernel written in Bass

```python
import math
import jax, jax.numpy as jnp
from jax import lax
import numpy as np

D_MODEL = 1024
BATCH = 8
SEQ = 8192
DEPTH = 2

CTX_LEN = 256
GRID_W = 64
EPS = 1e-6
ROPE_BASE = 10000.0
Q_BLOCK = 128
f32 = jnp.float32

H_A = 4
DH_A = 64
DV_A = 2 * DH_A
A_W = H_A * DV_A
H_B = 4
DH_B = 128
B_W = H_B * DH_B
GDN_CHUNK = 64
CONV_K = 5
H_C = 8
DH_C = 128
WIN_R = 8
WIN_C = 16
MIX_ODD = H_C * DH_C

MIX_EVEN = A_W + B_W
EVEN_IN = 3 * A_W + 4 * B_W + 4 * H_B
EVEN_CUTS = [A_W, 2 * A_W, 3 * A_W, 3 * A_W + 3 * B_W, 3 * A_W + 4 * B_W]
FF_HIDDEN = ((8 * D_MODEL // 3 + 255) // 256) * 256
N_EVEN = (DEPTH + 1) // 2
N_ODD = DEPTH // 2

kernel_name = "hybrid_diffattn_gdn_natten_trunk"


def rms_norm(x, gain):
    xf = x.astype(f32)
    y = xf * lax.rsqrt(jnp.mean(xf * xf, axis=-1, keepdims=True) + EPS)
    return (y * gain.astype(f32)).astype(x.dtype)


def l2_norm(x):
    return x * lax.rsqrt(jnp.sum(x * x, axis=-1, keepdims=True) + EPS)


def heads(t, n):
    b, l, _ = t.shape
    return t.reshape(b, l, n, -1).transpose(0, 2, 1, 3)


def merge_heads(t):
    b, h, l, d = t.shape
    return t.transpose(0, 2, 1, 3).reshape(b, l, h * d)


def axial_rope(l, dim):
    n_freq = dim // 4
    inv = ROPE_BASE ** (-jnp.arange(n_freq, dtype=f32) / n_freq)
    t = jnp.arange(l)
    row = (t // GRID_W).astype(f32)
    col = (t % GRID_W).astype(f32)
    ang = jnp.concatenate([row[:, None] * inv, col[:, None] * inv], axis=-1)
    return jnp.cos(ang), jnp.sin(ang)


def apply_rope(x, cos, sin):
    xf = x.astype(f32).reshape(*x.shape[:-1], x.shape[-1] // 2, 2)
    x0, x1 = xf[..., 0], xf[..., 1]
    out = jnp.stack([x0 * cos - x1 * sin, x0 * sin + x1 * cos], axis=-1)
    return out.reshape(x.shape).astype(x.dtype)


def softmax_attention(q, k, v):
    s = jnp.einsum('bhqd,bhkd->bhqk', q, k).astype(f32) * (q.shape[-1] ** -0.5)
    return jnp.einsum('bhqk,bhkd->bhqd', jax.nn.softmax(s, axis=-1).astype(v.dtype), v)


def over_query_blocks(fn, qs):
    b, h, l, _ = qs[0].shape
    nb = l // Q_BLOCK
    blocked = tuple(jnp.moveaxis(t.reshape(b, h, nb, Q_BLOCK, t.shape[-1]), 2, 0) for t in qs)
    out = lax.map(lambda qq: fn(*qq), blocked)
    return jnp.moveaxis(out, 0, 2).reshape(b, h, l, out.shape[-1])


def diff_core(q1, q2, k1, k2, v, lam):
    scale = DH_A ** -0.5
    p1 = jax.nn.softmax(jnp.einsum('bhqd,bhkd->bhqk', q1, k1).astype(f32) * scale, axis=-1)
    p2 = jax.nn.softmax(jnp.einsum('bhqd,bhkd->bhqk', q2, k2).astype(f32) * scale, axis=-1)
    return jnp.einsum('bhqk,bhkd->bhqd', (p1 - lam * p2).astype(v.dtype), v)


def diff_attention(q, k, v, qc, kc, vc, qk_gain, lam_vec, subln, lam_init, cos, sin, ctx_out):
    lv = lam_vec.astype(f32)
    lam = jnp.exp(jnp.sum(lv[0] * lv[1])) - jnp.exp(jnp.sum(lv[2] * lv[3])) + lam_init

    def split_maps(t, gain, rope):
        t = heads(t, H_A)
        b, h, l, _ = t.shape
        t = rms_norm(t.reshape(b, h, l, 2, DH_A), gain)
        t1, t2 = t[..., 0, :], t[..., 1, :]
        if rope:
            t1, t2 = apply_rope(t1, cos, sin), apply_rope(t2, cos, sin)
        return t1, t2

    q1, q2 = split_maps(q, qk_gain[0], True)
    k1, k2 = split_maps(k, qk_gain[1], True)
    k1c, k2c = split_maps(kc, qk_gain[1], False)
    vh, vch = heads(v, H_A), heads(vc, H_A)
    k1a = jnp.concatenate([k1c, k1], axis=2)
    k2a = jnp.concatenate([k2c, k2], axis=2)
    va = jnp.concatenate([vch, vh], axis=2)
    o = over_query_blocks(lambda a, b_: diff_core(a, b_, k1a, k2a, va, lam), (q1, q2))

    def post(t):
        return merge_heads(rms_norm(t, subln) * (1.0 - lam_init))

    oc = None
    if ctx_out:
        q1c, q2c = split_maps(qc, qk_gain[0], False)
        oc = post(diff_core(q1c, q2c, k1c, k2c, vch, lam))
    return post(o), oc


def short_conv(t, w):
    c = t.shape[-1]
    y = lax.conv_general_dilated(t, w[:, None, :].astype(t.dtype), window_strides=(1,),
                                 padding=[(CONV_K // 2, CONV_K // 2)],
                                 dimension_numbers=('NWC', 'WIO', 'NWC'), feature_group_count=c)
    return jax.nn.silu(y)


def gdn_inputs(qkv, gates, conv_w, a_log, dt_bias):
    y = short_conv(qkv, conv_w).astype(f32)
    q, k, v = jnp.split(y, 3, axis=-1)
    q = l2_norm(heads(q, H_B)) * (DH_B ** -0.5)
    k = l2_norm(heads(k, H_B))
    v = heads(v, H_B)
    b, l, _ = gates.shape
    g = gates.astype(f32).transpose(0, 2, 1)
    beta = jax.nn.sigmoid(g[:, :2 * H_B]).reshape(b, 2, H_B, l)
    a = g[:, 2 * H_B:].reshape(b, 2, H_B, l)
    log_alpha = -jnp.exp(a_log.astype(f32))[None, :, :, None] * jax.nn.softplus(a + dt_bias.astype(f32)[None, :, :, None])
    return q, k, v, log_alpha, beta


def gated_delta_chunked(q, k, v, log_a, beta, s0):
    b, h, l, dk = q.shape
    dv = v.shape[-1]
    n = l // GDN_CHUNK
    rs = lambda t: t.reshape(b, h, n, GDN_CHUNK, *t.shape[3:])
    q, k, v, log_a, beta = rs(q), rs(k), rs(v), rs(log_a), rs(beta)
    g = jnp.cumsum(log_a, axis=-1)
    idx = jnp.arange(GDN_CHUNK)
    causal = idx[:, None] >= idx[None, :]
    strict = idx[:, None] > idx[None, :]
    decay = jnp.exp(jnp.where(causal, g[..., :, None] - g[..., None, :], -jnp.inf))
    kk = jnp.einsum('bhncd,bhnsd->bhncs', k, k)
    lmat = jnp.where(strict, beta[..., :, None] * decay * kk, 0.0)
    rhs = jnp.concatenate([beta[..., None] * v, (beta * jnp.exp(g))[..., None] * k], axis=-1)
    sol = lax.linalg.triangular_solve(jnp.eye(GDN_CHUNK, dtype=f32) + lmat, rhs,
                                      left_side=True, lower=True, unit_diagonal=True)
    u0, w = sol[..., :dv], sol[..., dv:]
    aqk = jnp.einsum('bhncd,bhnsd->bhncs', q, k) * decay
    qg = q * jnp.exp(g)[..., None]
    kg = k * jnp.exp(g[..., -1:] - g)[..., None]
    gl = jnp.exp(g[..., -1])

    def step(s, inp):
        u0_c, w_c, aqk_c, qg_c, kg_c, gl_c = inp
        u = u0_c - w_c @ s
        o = qg_c @ s + aqk_c @ u
        s = gl_c[..., None, None] * s + jnp.swapaxes(kg_c, -1, -2) @ u
        return s, o

    xs = tuple(jnp.moveaxis(t, 2, 0) for t in (u0, w, aqk, qg, kg, gl))
    s_fin, o = lax.scan(step, s0, xs)
    return jnp.moveaxis(o, 0, 2).reshape(b, h, l, dv), s_fin


def bidirectional_delta(lat, ctx):
    q, k, v, la, be = lat
    qc, kc, vc, lac, bec = ctx
    s0 = jnp.zeros((q.shape[0], H_B, DH_B, DH_B), f32)
    fl = lambda t: jnp.flip(t, axis=2)
    oc_f, s_f = gated_delta_chunked(qc, kc, vc, lac[:, 0], bec[:, 0], s0)
    oc_b, s_b = gated_delta_chunked(fl(qc), fl(kc), fl(vc), fl(lac[:, 1]), fl(bec[:, 1]), s0)
    o_f, _ = gated_delta_chunked(q, k, v, la[:, 0], be[:, 0], s_f)
    o_b, _ = gated_delta_chunked(fl(q), fl(k), fl(v), fl(la[:, 1]), fl(be[:, 1]), s_b)
    return o_f + fl(o_b), oc_f + fl(oc_b)


def gated_deltanet(qkv, gate, gates, qkv_c, gate_c, gates_c, conv_w, a_log, dt_bias, gain, ctx_out):
    lat = gdn_inputs(qkv, gates, conv_w, a_log, dt_bias)
    ctxi = gdn_inputs(qkv_c, gates_c, conv_w, a_log, dt_bias)
    o_lat, o_ctx = bidirectional_delta(lat, ctxi)

    def post(o, gg):
        b, l, _ = gg.shape
        o = rms_norm(o.transpose(0, 2, 1, 3), gain) * jax.nn.silu(gg.astype(f32).reshape(b, l, H_B, DH_B))
        return o.reshape(b, l, B_W).astype(gg.dtype)

    return post(o_lat, gate), (post(o_ctx, gate_c) if ctx_out else None)


def even_mixer(h, hc, w_in, w_out, qk_gain, lam_vec, subln, lam_init, conv_w, a_log, dt_bias,
               gdn_gain, cos, sin, ctx_out):
    qa, ka, va, qkv_b, g_b, gates_b = jnp.split(h @ w_in, EVEN_CUTS, axis=-1)
    qac, kac, vac, qkv_bc, g_bc, gates_bc = jnp.split(hc @ w_in, EVEN_CUTS, axis=-1)
    a_lat, a_ctx = diff_attention(qa, ka, va, qac, kac, vac, qk_gain, lam_vec, subln, lam_init,
                                  cos, sin, ctx_out)
    b_lat, b_ctx = gated_deltanet(qkv_b, g_b, gates_b, qkv_bc, g_bc, gates_bc, conv_w, a_log,
                                  dt_bias, gdn_gain, ctx_out)
    out = jnp.concatenate([a_lat, b_lat], axis=-1) @ w_out
    out_c = jnp.concatenate([a_ctx, b_ctx], axis=-1) @ w_out if ctx_out else None
    return out, out_c


def neighbourhood_attention(q, k, v, k_ctx, v_ctx, rpb):
    b, h, l, d = q.shape
    rows = l // GRID_W
    wr = min(WIN_R, rows)
    scale = d ** -0.5
    grid = lambda t: t.reshape(b, h, rows, GRID_W, d)
    qg, kg, vg = grid(q), grid(k), grid(v)
    cols = np.arange(GRID_W)
    c_start = np.clip(cols - WIN_C // 2, 0, GRID_W - WIN_C)
    col_mask = (cols[None, :] >= c_start[:, None]) & (cols[None, :] < c_start[:, None] + WIN_C)
    dc_idx = np.clip(cols[None, :] - cols[:, None] + WIN_C - 1, 0, 2 * WIN_C - 2)
    rpb_c = rpb[:, :, dc_idx]

    def row_block(r):
        r_start = jnp.clip(r - wr // 2, 0, rows - wr)
        q_r = lax.dynamic_index_in_dim(qg, r, axis=2, keepdims=False)
        k_b = lax.dynamic_slice_in_dim(kg, r_start, wr, axis=2)
        v_b = lax.dynamic_slice_in_dim(vg, r_start, wr, axis=2)
        dr_idx = r_start + jnp.arange(wr) - r + WIN_R - 1
        bias = jnp.take(rpb_c, dr_idx, axis=1).transpose(0, 2, 1, 3)
        s_lat = jnp.einsum('bhqd,bhrkd->bhqrk', q_r, k_b).astype(f32) * scale + bias.astype(f32)
        s_lat = jnp.where(col_mask[:, None, :], s_lat, -jnp.inf).reshape(b, h, GRID_W, wr * GRID_W)
        s_ctx = jnp.einsum('bhqd,bhkd->bhqk', q_r, k_ctx).astype(f32) * scale
        p = jax.nn.softmax(jnp.concatenate([s_lat, s_ctx], axis=-1), axis=-1).astype(v.dtype)
        p_lat = p[..., :wr * GRID_W].reshape(b, h, GRID_W, wr, GRID_W)
        return (jnp.einsum('bhqrk,bhrkd->bhqd', p_lat, v_b)
                + jnp.einsum('bhqk,bhkd->bhqd', p[..., wr * GRID_W:], v_ctx))

    out = lax.map(row_block, jnp.arange(rows))
    return jnp.moveaxis(out, 0, 2).reshape(b, h, l, d)


def odd_mixer(h, hc, w_in, w_out, qk_gain, rpb, ctx_out):
    q, k, v = jnp.split(h @ w_in, 3, axis=-1)
    kc, vc = jnp.split(hc @ w_in[:, MIX_ODD:], 2, axis=-1)
    q, k, v = rms_norm(heads(q, H_C), qk_gain[0]), rms_norm(heads(k, H_C), qk_gain[1]), heads(v, H_C)
    kc, vc = rms_norm(heads(kc, H_C), qk_gain[1]), heads(vc, H_C)
    out = merge_heads(neighbourhood_attention(q, k, v, kc, vc, rpb)) @ w_out
    out_c = None
    if ctx_out:
        qc = rms_norm(heads(hc @ w_in[:, :MIX_ODD], H_C), qk_gain[0])
        out_c = merge_heads(softmax_attention(qc, kc, vc)) @ w_out
    return out, out_c


def swiglu(h, w_in, w_out):
    gt, up = jnp.split(h @ w_in, 2, axis=-1)
    return (jax.nn.silu(gt) * up) @ w_out


def setup_inputs(seed: int = 0) -> dict:
    key = jax.random.key(seed)
    ks = jax.random.split(key, 24)
    d = D_MODEL
    nrm = lambda k, shape, s: jax.random.normal(k, shape, f32) * s
    dt = jnp.exp(jax.random.uniform(ks[17], (N_EVEN, 2, H_B), f32, math.log(1e-3), math.log(1e-1)))
    return {
        "x": nrm(ks[0], (BATCH, SEQ, d), 1.0),
        "c": nrm(ks[1], (BATCH, d), 1.0),
        "ctx": nrm(ks[2], (BATCH, CTX_LEN, d), 1.0),
        "c_ctx": nrm(ks[3], (d,), 1.0),
        "ada_w": nrm(ks[4], (DEPTH, d, 6 * d), 0.5 * d ** -0.5),
        "ada_b": nrm(ks[5], (DEPTH, 6 * d), 0.01),
        "norm_mix": 1.0 + nrm(ks[6], (DEPTH, d), 0.02),
        "norm_ffn": 1.0 + nrm(ks[7], (DEPTH, d), 0.02),
        "ffn_w_in": nrm(ks[8], (DEPTH, d, 2 * FF_HIDDEN), d ** -0.5),
        "ffn_w_out": nrm(ks[9], (DEPTH, FF_HIDDEN, d), FF_HIDDEN ** -0.5),
        "even_w_in": nrm(ks[10], (N_EVEN, d, EVEN_IN), d ** -0.5),
        "even_w_out": nrm(ks[11], (N_EVEN, MIX_EVEN, d), MIX_EVEN ** -0.5),
        "diff_qk_gain": 1.0 + nrm(ks[12], (N_EVEN, 2, DH_A), 0.02),
        "diff_lambda": nrm(ks[13], (N_EVEN, 4, DH_A), 0.1),
        "diff_subln": 1.0 + nrm(ks[14], (N_EVEN, DV_A), 0.02),
        "gdn_conv": nrm(ks[15], (N_EVEN, CONV_K, 3 * B_W), CONV_K ** -0.5),
        "gdn_a_log": jnp.log(jax.random.uniform(ks[16], (N_EVEN, 2, H_B), f32, 1.0, 16.0)),
        "gdn_dt_bias": dt + jnp.log(-jnp.expm1(-dt)),
        "gdn_norm": 1.0 + nrm(ks[18], (N_EVEN, DH_B), 0.02),
        "odd_w_in": nrm(ks[19], (N_ODD, d, 3 * MIX_ODD), d ** -0.5),
        "odd_w_out": nrm(ks[20], (N_ODD, MIX_ODD, d), MIX_ODD ** -0.5),
        "na_qk_gain": 1.0 + nrm(ks[21], (N_ODD, 2, DH_C), 0.02),
        "na_rpb": nrm(ks[22], (N_ODD, H_C, 2 * WIN_R - 1, 2 * WIN_C - 1), 0.02),
    }


def reference(x, c, ctx, c_ctx, ada_w, ada_b, norm_mix, norm_ffn, ffn_w_in, ffn_w_out,
              even_w_in, even_w_out, diff_qk_gain, diff_lambda, diff_subln, gdn_conv, gdn_a_log,
              gdn_dt_bias, gdn_norm, odd_w_in, odd_w_out, na_qk_gain, na_rpb):
    cos, sin = axial_rope(x.shape[1], DH_A)
    silu_c = jax.nn.silu(c)
    silu_cc = jax.nn.silu(c_ctx)
    for l in range(DEPTH):
        ctx_out = l < DEPTH - 1
        m = silu_c @ ada_w[l] + ada_b[l]
        mc = silu_cc @ ada_w[l] + ada_b[l]
        sh_m, sc_m, g_m, sh_f, sc_f, g_f = [t[:, None, :] for t in jnp.split(m, 6, axis=-1)]
        csh_m, csc_m, cg_m, csh_f, csc_f, cg_f = jnp.split(mc, 6)
        h = rms_norm(x, norm_mix[l]) * (1.0 + sc_m) + sh_m
        hc = rms_norm(ctx, norm_mix[l]) * (1.0 + csc_m) + csh_m
        if l % 2 == 0:
            e = l // 2
            lam_init = 0.8 - 0.6 * math.exp(-0.3 * l)
            o, oc = even_mixer(h, hc, even_w_in[e], even_w_out[e], diff_qk_gain[e], diff_lambda[e],
                               diff_subln[e], lam_init, gdn_conv[e], gdn_a_log[e], gdn_dt_bias[e],
                               gdn_norm[e], cos, sin, ctx_out)
        else:
            od = l // 2
            o, oc = odd_mixer(h, hc, odd_w_in[od], odd_w_out[od], na_qk_gain[od], na_rpb[od], ctx_out)
        x = x + g_m * o
        hf = rms_norm(x, norm_ffn[l]) * (1.0 + sc_f) + sh_f
        x = x + g_f * swiglu(hf, ffn_w_in[l], ffn_w_out[l])
        if ctx_out:
            ctx = ctx + cg_m * oc
            hcf = rms_norm(ctx, norm_ffn[l]) * (1.0 + csc_f) + csh_f
            ctx = ctx + cg_f * swiglu(hcf, ffn_w_in[l], ffn_w_out[l])
    return x
```

```python
import numpy as np
import concourse.bass as bass
import concourse.mybir as mybir
from concourse.bass_utils import run_bass_kernel_spmd
from contextlib import ExitStack

F32 = mybir.dt.float32
BF16 = mybir.dt.bfloat16
AF = mybir.ActivationFunctionType
ALU = mybir.AluOpType
AX = mybir.AxisListType
MM = BF16

D = 1024
FF = 2816
EVEN_IN = 3600
EPS = 1e-6
DMA_POOL = 24
QUEUES = ("pe", "act", "dve", "pool", "sp")


class Buf:
    __slots__ = ("w", "r", "name")
    def __init__(self, name=""):
        self.w = []; self.r = []; self.name = name


class Tile(Buf):
    __slots__ = ("t",)
    def __init__(self, t, name=""):
        Buf.__init__(self, name); self.t = t
    def __getitem__(self, k):
        return self.t[k]


class Op:
    __slots__ = ("q", "fn", "deps", "dma", "sig", "sem", "val", "idx")


class Phase:
    def __init__(self, nc, name):
        self.nc = nc; self.name = name; self.ops = []; self.es = ExitStack(); self.ntile = 0; self.touched = []

    def sb(self, shape, dt, name=None):
        self.ntile += 1
        name = name or f"t{self.ntile}"
        t = self.es.enter_context(self.nc.sbuf_tensor(f"{self.name}_{name}", list(shape), dt))
        return Tile(t, name)

    def ps(self, shape, dt, name=None):
        self.ntile += 1
        name = name or f"p{self.ntile}"
        t = self.es.enter_context(self.nc.psum_tensor(f"{self.name}_{name}", list(shape), dt))
        return Tile(t, name)

    def op(self, q, fn, reads=(), writes=(), dma=False):
        o = Op()
        o.q = q; o.fn = fn; o.dma = dma; o.sig = dma; o.sem = None; o.val = 0
        o.idx = len(self.ops)
        deps = set()
        for b in reads:
            deps.update(b.w)
        for b in writes:
            deps.update(b.w); deps.update(b.r)
        o.deps = deps
        self.touched.extend(reads); self.touched.extend(writes)
        for b in reads:
            b.r.append(o.idx)
        for b in writes:
            b.w = [o.idx]; b.r = []
        self.ops.append(o)
        return o

    def x(self, q, meth, reads, writes, *args, **kw):
        return self.op(q, lambda e: getattr(e, meth)(*args, **kw), reads, writes)

    def mm(self, out_t, out_ap, l_t, l_ap, r_t, r_ap, start, stop):
        return self.x("pe", "matmul", [l_t, r_t], [out_t], out_ap, lhsT=l_ap, rhs=r_ap, start=start, stop=stop)

    def pe(self, fn, reads=(), writes=()): return self.op("pe", fn, reads, writes)
    def act(self, fn, reads=(), writes=()): return self.op("act", fn, reads, writes)
    def dve(self, fn, reads=(), writes=()): return self.op("dve", fn, reads, writes)
    def pool(self, fn, reads=(), writes=()): return self.op("pool", fn, reads, writes)
    def dma(self, out, in_, reads=(), writes=(), q="sp", **kw):
        return self.op(q, lambda e: e.dma_start(out=out, in_=in_, **kw), reads, writes, dma=True)

    def emit(self):
        nc = self.nc; ops = self.ops
        def skip(do, o):
            return (not do.dma) and (not o.dma) and do.q == o.q and o.q == "pe"
        for o in ops:
            for d in o.deps:
                do = ops[d]
                if do.dma or skip(do, o):
                    continue
                do.sig = True
        st = getattr(nc, "_semstate", None)
        if st is None:
            st = {"es": ExitStack(), "esem": {}, "dsem": {q: [] for q in QUEUES}, "cnt": {q: 0 for q in QUEUES},
                  "dcnt": {q: 0 for q in QUEUES}}
            for q in QUEUES:
                st["esem"][q] = st["es"].enter_context(nc.semaphore(f"s_{q}"))
            nc._semstate = st
        esem = st["esem"]; dsem = st["dsem"]; cnt = st["cnt"]; dcnt = st["dcnt"]
        final_dma = {}
        for o in ops:
            if o.dma:
                k = dcnt[o.q]; dcnt[o.q] += 1
                if k < DMA_POOL:
                    dsem[o.q].append(st["es"].enter_context(nc.semaphore(f"d_{o.q}{k}")))
                o.sem = dsem[o.q][k % DMA_POOL]
                o.val = 16 * (k // DMA_POOL + 1)
                final_dma[id(o.sem)] = (o.sem, o.val)
            elif o.sig:
                cnt[o.q] += 1
                o.sem = esem[o.q]; o.val = cnt[o.q]
        byq = {q: [o for o in ops if o.q == q] for q in QUEUES}
        engs = {"pe": "tensor", "act": "scalar", "dve": "vector", "pool": "gpsimd", "sp": "sync"}
        with nc.Block() as block:
            for q in QUEUES:
                def body(e, lst=byq[q], q=q):
                    known = {}
                    def wait(sem, val):
                        if known.get(id(sem), 0) >= val:
                            return
                        e.wait_ge(sem, val)
                        known[id(sem)] = val
                    for o in lst:
                        for d in sorted(o.deps):
                            do = ops[d]
                            if skip(do, o):
                                continue
                            wait(do.sem, do.val)
                        if o.dma and o.val > 16:
                            wait(o.sem, o.val - 16)
                        ins = o.fn(e)
                        if o.sig:
                            ins.then_inc(o.sem, 16 if o.dma else 1)
                    if q == "sp":
                        for sem, val in final_dma.values():
                            wait(sem, val)
                getattr(block, engs[q])(body)
        self.es.close()
        for b in self.touched:
            b.w = []; b.r = []


def make_ident(P, dt=MM):
    ident = P.sb([128, 128], dt, "ident")
    P.pool(lambda e: e.memset(ident[:], 1.0), writes=[ident])
    P.pool(lambda e: e.affine_select(out=ident[:], in_=ident[:], pattern=[[-1, 128]], compare_op=ALU.is_equal,
                                     fill=0.0, base=0, channel_multiplier=1), reads=[ident], writes=[ident])
    return ident


class Ctx:
    pass


def phase_adaln(nc, C):
    P = Phase(nc, "ada")
    cs = P.sb([128, 2, 8], F32, "cs")
    craw = P.sb([128, 2, 8], F32, "craw")
    P.dma(craw[:, 0, :], C.c.rearrange("(p j) -> p j", j=8), writes=[craw])
    P.dma(craw[:, 1, :], C.c_ctx.rearrange("(p j) -> p j", j=8), writes=[craw])
    P.act(lambda e: e.activation(out=cs[:], in_=craw[:], func=AF.Silu), reads=[craw], writes=[cs])
    wt = [P.sb([128, 8, 512], F32, f"w{i}") for i in range(2)]
    acc = [P.ps([2, 512], F32, f"acc{i}") for i in range(2)]
    ms = [P.sb([2, 512], F32, f"ms{i}") for i in range(2)]
    it = 0
    for l in range(2):
        bias = P.sb([2, 6144], F32, f"bias{l}")
        P.dma(bias[:], C.ada_b[l, :].partition_broadcast(2), writes=[bias])
        wv = C.ada_w[l].rearrange("(p j) n -> p j n", j=8)
        for ch in range(12):
            w = wt[it % 2]; a = acc[it % 2]; m = ms[it % 2]; it += 1
            P.dma(w[:], wv[:, :, ch * 512:(ch + 1) * 512], writes=[w])
            for j in range(8):
                P.mm(a, a[:], cs, cs[:, :, j], w, w[:, j, :], j == 0, j == 7)
            P.x("dve", "tensor_tensor", [a, bias], [m], out=m[:], in0=a[:], in1=bias[:, ch * 512:(ch + 1) * 512], op=ALU.add)
            P.dma(C.mod[l][:, ch * 512:(ch + 1) * 512], m[:], reads=[m], writes=[C.mod_b[l]])
    P.emit()


def load_mod(P, C, l, which, row, name):
    t = P.sb([128, D], F32, name)
    P.dma(t[:], C.mod[l][row, which * D:(which + 1) * D].partition_broadcast(128), reads=[C.mod_b[l]], writes=[t])
    return t


def load_row(P, ap_row, n, name):
    t = P.sb([128, n], F32, name)
    P.dma(t[:], ap_row.partition_broadcast(128), writes=[t])
    return t


def norm_mod_tile(P, C, xt, hb, G, SH, scr, ss, junk):
    P.act(lambda e: e.activation(out=junk[:], in_=xt[:], func=AF.Square, accum_out=ss[:]), reads=[xt], writes=[junk, ss])
    P.dve(lambda e: e.tensor_scalar(out=ss[:], in0=ss[:], scalar1=1.0 / D, scalar2=EPS, op0=ALU.mult, op1=ALU.add), reads=[ss], writes=[ss])
    P.act(lambda e: e.activation(out=ss[:], in_=ss[:], func=AF.Sqrt), reads=[ss], writes=[ss])
    P.dve(lambda e: e.reciprocal(out=ss[:], in_=ss[:]), reads=[ss], writes=[ss])
    P.dve(lambda e: e.scalar_tensor_tensor(out=scr[:], in0=xt[:], scalar=ss[:], in1=G[:], op0=ALU.mult, op1=ALU.mult),
          reads=[xt, ss, G], writes=[scr])
    P.dve(lambda e: e.tensor_tensor(out=hb[:], in0=scr[:], in1=SH[:], op=ALU.add), reads=[scr, SH], writes=[hb])


def make_G(P, C, l, sc_idx, row, gain_row, name):
    sc = load_mod(P, C, l, sc_idx, row, name + "_sc")
    gn = load_row(P, gain_row, D, name + "_gn")
    P.dve(lambda e: e.scalar_tensor_tensor(out=sc[:], in0=sc[:], scalar=1.0, in1=gn[:], op0=ALU.add, op1=ALU.mult),
          reads=[sc, gn], writes=[sc])
    return sc


def load_w_bf16(P, wt, w_dram, K, N, splitn=1800):
    wv = w_dram.rearrange("(j p) n -> p j n", p=128)
    for j in range(K // 128):
        for n0 in range(0, N, splitn):
            n1 = min(N, n0 + splitn)
            P.dma(wt[:, j, n0:n1], wv[:, j, n0:n1], writes=[wt], q="pool")


def transpose_tile(P, hT, col0, hb, ident, tp, nblk=8):
    for j in range(nblk):
        P.x("pe", "transpose", [hb, ident], [tp], tp[:, j, :], hb[:, j * 128:(j + 1) * 128], ident[:])
    P.x("act", "copy", [tp], [hT], out=hT[:, 0:nblk, col0:col0 + 128], in_=tp[:, 0:nblk, :])


def phase_proj(nc, C, name, l, w_dram, N, groups, tm_cols, tm_dsts, fm_c0, fm_nblk, fm_dst):
    P = Phase(nc, name)
    ident = make_ident(P)
    wt = P.sb([128, 8, N], MM, "w")
    load_w_bf16(P, wt, w_dram, D, N, splitn=(1800 if N == EVEN_IN else 1536))
    OW = sum(c1 - c0 for (c0, c1) in tm_cols)
    Gs = [make_G(P, C, l, 1, r, C.norm_mix[l, :], f"G{r}") for r in range(2)]
    SHs = [load_mod(P, C, l, 0, r, f"SH{r}") for r in range(2)]
    xts = [P.sb([128, D], F32, f"x{i}") for i in range(2)]
    scr = P.sb([128, D], F32, "scr"); junk = P.sb([128, D], F32, "junk")
    sss = [P.sb([128, 1], F32, f"ss{i}") for i in range(2)]
    hbs = [P.sb([128, D], MM, f"hb{i}") for i in range(2)]
    tps = [P.ps([128, 8, 128], MM, f"tp{i}") for i in range(2)]
    hT = [P.sb([128, 8, 512], MM, f"hT{i}") for i in range(2)]
    acc = [P.ps([128, 512], F32, f"acc{i}") for i in range(4)]
    otm = [P.sb([128, OW], F32, f"otm{i}") for i in range(2)]
    ofm = [P.sb([128, 512], F32, f"ofm{i}") for i in range(3)]
    it = 0; ia = 0; io = 0
    for gi, (row, src, t0, nt, a0) in enumerate(groups):
        hTg = hT[gi % 2]
        ntile = nt // 128
        for ti in range(ntile):
            xt = xts[it % 2]; hb = hbs[it % 2]; ss = sss[it % 2]; tp = tps[it % 2]; it += 1
            P.dma(xt[:], src[t0 + ti * 128:t0 + (ti + 1) * 128, :], writes=[xt])
            norm_mod_tile(P, C, xt, hb, Gs[row], SHs[row], scr, ss, junk)
            transpose_tile(P, hTg, ti * 128, hb, ident, tp)
        for ti in range(ntile):
            o = otm[io % 2]; io += 1
            oc = 0
            for (c0, c1) in tm_cols:
                a = acc[ia % 4]; ia += 1
                w_ = c1 - c0
                for j in range(8):
                    P.mm(a, a[:, 0:w_], hTg, hTg[:, j, ti * 128:(ti + 1) * 128], wt, wt[:, j, c0:c1], j == 0, j == 7)
                P.x("dve", "tensor_copy", [a], [o], out=o[:, oc:oc + w_], in_=a[:, 0:w_])
                oc += w_
            r0 = a0 + ti * 128
            for (dd, oc0, oc1) in tm_dsts:
                P.dma(dd[r0:r0 + 128, :], o[:, oc0:oc1], reads=[o])
        for cb in range(fm_nblk):
            a = acc[ia % 4]; ia += 1
            o = ofm[cb % 3]
            c0 = fm_c0 + cb * 128
            for j in range(8):
                P.mm(a, a[:, 0:nt], wt, wt[:, j, c0:c0 + 128], hTg, hTg[:, j, 0:nt], j == 0, j == 7)
            P.x("act", "copy", [a], [o], out=o[:, 0:nt], in_=a[:, 0:nt])
            P.dma(fm_dst[cb * 128:(cb + 1) * 128, a0:a0 + nt], o[:, 0:nt], reads=[o])
    P.emit()


LAM_INIT0 = 0.2


def rms_groups(P, v3, ss, sq3, ngrp, gd, gain):
    t = v3_t = None
    return None


def phase_da_prep(nc, C):
    P = Phase(nc, "dap")
    ident = make_ident(P)
    gains = [load_row(P, C.diff_qk_gain[0, i, :], 64, f"g{i}") for i in range(2)]
    lv = P.sb([128, 4, 64], F32, "lv")
    for i in range(4):
        P.dma(lv[:, i, :], C.diff_lambda[0, i, :].partition_broadcast(128), writes=[lv])
    pr = P.sb([128, 2, 64], F32, "pr"); s2 = P.sb([128, 2], F32, "s2"); nl = P.sb([128, 1], F32, "nl")
    P.x("dve", "tensor_tensor", [lv], [pr], out=pr[:, 0, :], in0=lv[:, 0, :], in1=lv[:, 1, :], op=ALU.mult)
    P.x("dve", "tensor_tensor", [lv], [pr], out=pr[:, 1, :], in0=lv[:, 2, :], in1=lv[:, 3, :], op=ALU.mult)
    P.x("dve", "tensor_reduce", [pr], [s2], out=s2[:], in_=pr[:], axis=AX.X, op=ALU.add)
    P.x("act", "activation", [s2], [s2], out=s2[:], in_=s2[:], func=AF.Exp)
    P.x("dve", "tensor_tensor", [s2], [nl], out=nl[:], in0=s2[:, 1:2], in1=s2[:, 0:1], op=ALU.subtract)
    P.x("dve", "tensor_scalar", [nl], [nl], out=nl[:], in0=nl[:], scalar1=-LAM_INIT0, scalar2=None, op0=ALU.add)
    P.dma(C.neglam[:, :], nl[:], reads=[nl], writes=[C.neglam_b])
    pts = [P.sb([128, 1536], F32, f"pt{i}") for i in range(2)]
    cs = [P.sb([128, 2, 32], F32, f"cs{i}") for i in range(2)]
    sq = P.sb([128, 512], F32, "sq"); ss = P.sb([128, 8], F32, "ss")
    t1 = P.sb([128, 256], F32, "t1"); t2 = P.sb([128, 256], F32, "t2")
    qn = P.sb([128, 512], F32, "qn")
    qb = [P.sb([128, 512], MM, f"qb{i}") for i in range(2)]
    vb = [P.sb([128, 512], MM, f"vb{i}") for i in range(2)]
    tp = [P.ps([128, 4, 128], MM, f"tp{i}") for i in range(2)]
    oT = [P.sb([128, 4, 128], MM, f"oT{i}") for i in range(2)]
    ib = 0
    for ti in range(C.A // 128):
        pt = pts[ti % 2]; c_ = cs[ti % 2]
        r0 = ti * 128
        lat = r0 >= C.CTX
        P.dma(pt[:], C.pa[r0:r0 + 128, :], writes=[pt])
        if lat:
            P.dma(c_[:, 0, :], C.rope_cos[r0 - C.CTX:r0 - C.CTX + 128, :], writes=[c_])
            P.dma(c_[:, 1, :], C.rope_sin[r0 - C.CTX:r0 - C.CTX + 128, :], writes=[c_])
        for qi in range(2):
            off = qi * 512
            v3 = pt[:, off:off + 512].rearrange("p (g d) -> p g d", d=64)
            sq3 = sq[:, :].rearrange("p (g d) -> p g d", d=64)
            qn3 = qn[:, :].rearrange("p (g d) -> p g d", d=64)
            P.x("dve", "tensor_tensor", [pt], [sq], out=sq3, in0=v3, in1=v3, op=ALU.mult)
            P.x("dve", "tensor_reduce", [sq], [ss], out=ss[:], in_=sq3, axis=AX.X, op=ALU.add)
            P.x("dve", "tensor_scalar", [ss], [ss], out=ss[:], in0=ss[:], scalar1=1.0 / 64, scalar2=EPS, op0=ALU.mult, op1=ALU.add)
            P.x("act", "activation", [ss], [ss], out=ss[:], in_=ss[:], func=AF.Sqrt)
            P.x("dve", "reciprocal", [ss], [ss], out=ss[:], in_=ss[:])
            P.x("dve", "tensor_tensor", [pt, ss], [qn], out=qn3, in0=v3, in1=ss[:, :].unsqueeze(2).to_broadcast([128, 8, 64]), op=ALU.mult)
            b = qb[ib % 2]; ib += 1
            g_b = gains[qi][:, :].unsqueeze(1).to_broadcast([128, 8, 64])
            if not lat:
                P.x("dve", "tensor_tensor", [qn, gains[qi]], [b], out=b[:, :].rearrange("p (g d) -> p g d", d=64), in0=qn3, in1=g_b, op=ALU.mult)
            else:
                P.x("dve", "tensor_tensor", [qn, gains[qi]], [qn], out=qn3, in0=qn3, in1=g_b, op=ALU.mult)
                q4 = qn[:, :].rearrange("p (g i two) -> p g i two", g=8, two=2)
                b4 = b[:, :].rearrange("p (g i two) -> p g i two", g=8, two=2)
                x0 = q4[:, :, :, 0]; x1 = q4[:, :, :, 1]
                cb = c_[:, 0, :].unsqueeze(1).to_broadcast([128, 8, 32]); sb_ = c_[:, 1, :].unsqueeze(1).to_broadcast([128, 8, 32])
                t13 = t1[:, :].rearrange("p (g i) -> p g i", g=8); t23 = t2[:, :].rearrange("p (g i) -> p g i", g=8)
                P.x("dve", "tensor_tensor", [qn, c_], [t1], out=t13, in0=x0, in1=cb, op=ALU.mult)
                P.x("dve", "tensor_tensor", [qn, c_], [t2], out=t23, in0=x1, in1=sb_, op=ALU.mult)
                P.x("dve", "tensor_tensor", [t1, t2], [b], out=b4[:, :, :, 0], in0=t13, in1=t23, op=ALU.subtract)
                P.x("dve", "tensor_tensor", [qn, c_], [t1], out=t13, in0=x0, in1=sb_, op=ALU.mult)
                P.x("dve", "tensor_tensor", [qn, c_], [t2], out=t23, in0=x1, in1=cb, op=ALU.mult)
                P.x("dve", "tensor_tensor", [t1, t2], [b], out=b4[:, :, :, 1], in0=t13, in1=t23, op=ALU.add)
            k = ib % 2
            transpose_tile(P, oT[k], 0, b, ident, tp[k], nblk=4)
            dstT = C.qT if qi == 0 else C.kT
            P.dma(dstT[:, r0:r0 + 128].rearrange("(h p) t -> p h t", p=128), oT[k][:], reads=[oT[k]],
                  writes=[(C.qT_b if qi == 0 else C.kT_b)[ti]])
        v_ = vb[ti % 2]
        P.x("act", "copy", [pt], [v_], out=v_[:], in_=pt[:, 1024:1536])
        P.dma(C.vtok[r0:r0 + 128, :], v_[:], reads=[v_], writes=[C.vtok_b[ti]])
    P.emit()


def phase_da_core(nc, C):
    P = Phase(nc, "dac")
    A = C.A; NKT = A // 128
    ones = P.sb([128, 128], MM, "ones")
    P.x("pool", "memset", [], [ones], ones[:], 1.0)
    onesf = P.sb([128, 128], F32, "onesf")
    P.x("pool", "memset", [], [onesf], onesf[:], 1.0 / 128)
    nl = P.sb([128, 1], F32, "nl")
    P.dma(nl[:], C.neglam[:, :], reads=[C.neglam_b], writes=[nl])
    sub = P.sb([128, 1], F32, "sub")
    P.dma(sub[:], C.diff_subln[0, :].rearrange("(p o) -> p o", o=1), writes=[sub])
    P.x("dve", "tensor_scalar", [sub], [sub], out=sub[:], in0=sub[:], scalar1=1.0 - LAM_INIT0, scalar2=None, op0=ALU.mult)
    kT = P.sb([128, A], MM, "kT"); vt = P.sb([128, NKT, 128], MM, "vt")
    qTs = [P.sb([128, 512], MM, f"qT{i}") for i in range(2)]
    sps = [P.ps([128, 512], F32, f"s{i}") for i in range(4)]
    ops_ = [P.ps([128, 512], F32, f"oacc{i}") for i in range(2)]
    zps = [P.ps([128, 512], F32, f"z{i}") for i in range(2)]
    pTs = [P.sb([128, 512], MM, f"pT{i}") for i in range(4)]
    r1 = P.sb([128, 512], F32, "r1"); r2 = P.sb([128, 512], F32, "r2"); o1 = P.sb([128, 512], F32, "o1")
    o2 = P.sb([128, 512], F32, "o2"); osq = P.sb([128, 512], F32, "osq"); ob = [P.sb([128, 512], MM, f"ob{i}") for i in range(2)]
    scale = 64 ** -0.5
    chunks = []
    for t0 in range(0, C.CTX, 512):
        chunks.append((t0, min(512, C.CTX - t0), C.CTX // 128))
    for t0 in range(0, C.T, 512):
        chunks.append((C.CTX + t0, min(512, C.T - t0), NKT))
    ic = 0; isp = 0
    for h in range(4):
        P.dma(kT[:], C.kT[h * 128:(h + 1) * 128, :], reads=C.kT_b, writes=[kT])
        P.dma(vt[:], C.vtok[:, h * 128:(h + 1) * 128].rearrange("(n p) d -> p n d", p=128), reads=C.vtok_b, writes=[vt])
        for (a0, nq, nkt) in chunks:
            q = qTs[ic % 2]; ic += 1
            P.dma(q[:, 0:nq], C.qT[h * 128:(h + 1) * 128, a0:a0 + nq], reads=[C.qT_b[(a0 + i * 128) // 128] for i in range(nq // 128)], writes=[q])
            for kt in range(nkt):
                for j in range(2):
                    sp = sps[isp % 4]; pT = pTs[isp % 4]; isp += 1
                    P.mm(sp, sp[:, 0:nq], kT, kT[64 * j:64 * j + 64, kt * 128:(kt + 1) * 128], q, q[64 * j:64 * j + 64, 0:nq], True, True)
                    P.x("act", "activation", [sp], [pT], out=pT[:, 0:nq], in_=sp[:, 0:nq], func=AF.Exp, scale=scale)
                    P.mm(ops_[j], ops_[j][:, 0:nq], vt, vt[:, kt, :], pT, pT[:, 0:nq], kt == 0, kt == nkt - 1)
                    P.mm(zps[j], zps[j][:, 0:nq], ones, ones[:], pT, pT[:, 0:nq], kt == 0, kt == nkt - 1)
            P.x("dve", "reciprocal", [zps[0]], [r1], out=r1[:, 0:nq], in_=zps[0][:, 0:nq])
            P.x("dve", "reciprocal", [zps[1]], [r2], out=r2[:, 0:nq], in_=zps[1][:, 0:nq])
            P.x("dve", "tensor_tensor", [ops_[0], r1], [o1], out=o1[:, 0:nq], in0=ops_[0][:, 0:nq], in1=r1[:, 0:nq], op=ALU.mult)
            P.x("dve", "tensor_tensor", [ops_[1], r2], [o2], out=o2[:, 0:nq], in0=ops_[1][:, 0:nq], in1=r2[:, 0:nq], op=ALU.mult)
            P.x("dve", "scalar_tensor_tensor", [o2, nl, o1], [o1], out=o1[:, 0:nq], in0=o2[:, 0:nq], scalar=nl[:], in1=o1[:, 0:nq], op0=ALU.mult, op1=ALU.add)
            P.x("dve", "tensor_tensor", [o1], [osq], out=osq[:, 0:nq], in0=o1[:, 0:nq], in1=o1[:, 0:nq], op=ALU.mult)
            zp = zps[0]
            P.mm(zp, zp[:, 0:nq], onesf, onesf[:], osq, osq[:, 0:nq], True, True)
            P.x("dve", "tensor_scalar", [zp], [r1], out=r1[:, 0:nq], in0=zp[:, 0:nq], scalar1=EPS, scalar2=None, op0=ALU.add)
            P.x("act", "activation", [r1], [r1], out=r1[:, 0:nq], in_=r1[:, 0:nq], func=AF.Sqrt)
            P.x("dve", "reciprocal", [r1], [r1], out=r1[:, 0:nq], in_=r1[:, 0:nq])
            o_ = ob[ic % 2]
            P.x("dve", "scalar_tensor_tensor", [o1, sub, r1], [o_], out=o_[:, 0:nq], in0=o1[:, 0:nq], scalar=sub[:], in1=r1[:, 0:nq], op0=ALU.mult, op1=ALU.mult)
            P.dma(C.mixT[h * 128:(h + 1) * 128, a0:a0 + nq], o_[:, 0:nq], reads=[o_], writes=[C.mixT_b[h][a0 // 512 if a0 >= C.CTX else 0]])
    P.emit()


def phase_gdn_prep(nc, C):
    P = Phase(nc, "gp")
    ident = make_ident(P, F32)
    onesf = P.sb([128, 128], F32, "onesf")
    P.x("pool", "memset", [], [onesf], onesf[:], 1.0)
    xin = [P.sb([128, 516], F32, f"xin{i}") for i in range(2)]
    accs = [P.sb([128, 512], F32, f"acc{i}") for i in range(2)]
    ys = [P.sb([128, 512], F32, f"y{i}") for i in range(2)]
    sq = P.sb([128, 512], F32, "sq"); rr = P.sb([128, 512], F32, "rr")
    pss = [P.ps([128, 512], F32, f"ps{i}") for i in range(2)]
    tps = [P.ps([128, 128], F32, f"tp{i}") for i in range(2)]
    tts = [P.sb([128, 128], F32, f"tt{i}") for i in range(3)]
    segs = [(0, C.CTX), (C.CTX, C.A)]
    it = 0; itp = 0
    for cb in range(12):
        kind = cb // 4; h = cb % 4
        wc = P.sb([128, 5], F32, f"wc{cb}")
        for k in range(5):
            P.dma(wc[:, k:k + 1], C.gdn_conv[0, k, cb * 128:(cb + 1) * 128].rearrange("(p o) -> p o", o=1), writes=[wc])
        for (s0, s1) in segs:
            for t0 in range(s0, s1, 512):
                nt = min(512, s1 - t0)
                x = xin[it % 2]; a = accs[it % 2]; y = ys[it % 2]; ps = pss[it % 2]; it += 1
                lo = max(s0, t0 - 2); hi = min(s1, t0 + nt + 2)
                if lo > t0 - 2 or hi < t0 + nt + 2:
                    P.x("pool", "memset", [], [x], x[:, 0:nt + 4], 0.0)
                P.dma(x[:, lo - (t0 - 2):hi - (t0 - 2)], C.qkvT[cb * 128:(cb + 1) * 128, lo:hi], writes=[x])
                P.x("dve", "tensor_scalar", [x, wc], [a], out=a[:, 0:nt], in0=x[:, 0:nt], scalar1=wc[:, 0:1], scalar2=None, op0=ALU.mult)
                for k in range(1, 5):
                    P.x("dve", "scalar_tensor_tensor", [x, wc, a], [a], out=a[:, 0:nt], in0=x[:, k:k + nt], scalar=wc[:, k:k + 1],
                        in1=a[:, 0:nt], op0=ALU.mult, op1=ALU.add)
                P.x("act", "activation", [a], [y], out=y[:, 0:nt], in_=a[:, 0:nt], func=AF.Silu)
                if kind < 2:
                    P.x("dve", "tensor_tensor", [y], [sq], out=sq[:, 0:nt], in0=y[:, 0:nt], in1=y[:, 0:nt], op=ALU.mult)
                    P.mm(ps, ps[:, 0:nt], onesf, onesf[:], sq, sq[:, 0:nt], True, True)
                    P.x("dve", "tensor_scalar", [ps], [rr], out=rr[:, 0:nt], in0=ps[:, 0:nt], scalar1=EPS, scalar2=None, op0=ALU.add)
                    P.x("act", "activation", [rr], [rr], out=rr[:, 0:nt], in_=rr[:, 0:nt], func=AF.Sqrt)
                    P.x("dve", "reciprocal", [rr], [rr], out=rr[:, 0:nt], in_=rr[:, 0:nt])
                    P.x("dve", "scalar_tensor_tensor", [y, rr], [y], out=y[:, 0:nt], in0=y[:, 0:nt],
                        scalar=(128 ** -0.5 if kind == 0 else 1.0), in1=rr[:, 0:nt], op0=ALU.mult, op1=ALU.mult)
                    P.dma(C.gqkT[kind][h * 128:(h + 1) * 128, t0:t0 + nt], y[:, 0:nt], reads=[y])
                if kind >= 1:
                    for i in range(nt // 128):
                        tp = tps[itp % 2]; tt = tts[itp % 3]; itp += 1
                        P.x("pe", "transpose", [y, ident], [tp], tp[:], y[:, i * 128:(i + 1) * 128], ident[:])
                        P.x("act", "copy", [tp], [tt], out=tt[:], in_=tp[:])
                        P.dma(C.gtok[kind - 1][t0 + i * 128:t0 + (i + 1) * 128, h * 128:(h + 1) * 128], tt[:], reads=[tt])
    P.emit()


def phase_gdn_main(nc, C):
    P = Phase(nc, "gm")
    ident = make_ident(P, F32)
    def tri(name, upper, strict):
        t = P.sb([128, 128], F32, name)
        P.x("pool", "memset", [], [t], t[:], 1.0)
        op = ALU.is_gt if strict else ALU.is_ge
        if upper:
            P.x("pool", "affine_select", [t], [t], out=t[:], in_=t[:], pattern=[[1, 128]], compare_op=op, fill=0.0, base=0, channel_multiplier=-1)
            P.x("pool", "memset", [t], [t], t[0:64, 64:128], 0.0)
        else:
            P.x("pool", "affine_select", [t], [t], out=t[:], in_=t[:], pattern=[[-1, 128]], compare_op=op, fill=0.0, base=0, channel_multiplier=1)
            P.x("pool", "memset", [t], [t], t[64:128, 0:64], 0.0)
        return t
    U = tri("U", True, False); L = tri("L", False, False); SU = tri("SU", True, True); SL = tri("SL", False, True)
    MinclT = [U, L]; Mstrict = [SL, SU]; MstrictT = [SU, SL]
    dtb = P.sb([128, 8], F32, "dtb"); negA = P.sb([128, 8], F32, "negA")
    P.dma(dtb[:], C.gdn_dt_bias[0].rearrange("a b -> (a b)").partition_broadcast(128), writes=[dtb])
    P.dma(negA[:], C.gdn_a_log[0].rearrange("a b -> (a b)").partition_broadcast(128), writes=[negA])
    P.x("act", "activation", [negA], [negA], out=negA[:], in_=negA[:], func=AF.Exp)
    P.x("dve", "tensor_scalar", [negA], [negA], out=negA[:], in0=negA[:], scalar1=-1.0, scalar2=None, op0=ALU.mult)
    S = [[P.sb([128, 128], F32, f"S{d}{h}") for h in range(4)] for d in range(2)]
    QT = [P.sb([128, 4, 128], F32, f"QT{i}") for i in range(2)]
    KT = [P.sb([128, 4, 128], F32, f"KT{i}") for i in range(2)]
    Kt = [P.sb([128, 512], F32, f"Kt{i}") for i in range(2)]
    Vt = [P.sb([128, 512], F32, f"Vt{i}") for i in range(2)]
    gt = [P.sb([128, 16], F32, f"gt{i}") for i in range(2)]
    beta = P.sb([128, 8], F32, "beta"); nbeta = P.sb([128, 8], F32, "nbeta"); la = P.sb([128, 8], F32, "la")
    ee = P.sb([128, 8], F32, "ee")
    sbn = {}
    def T_(name, shape=(128, 128)):
        sbn[name] = P.sb(list(shape), F32, name); return sbn[name]
    la_bc = T_("la_bc"); be_bc = T_("be_bc"); Grow = T_("Grow"); gcol = T_("gcol", (128, 1)); ngcol = T_("ngcol", (128, 1))
    decT = T_("decT"); t1 = T_("t1"); B0 = T_("B0"); aqkT = T_("aqkT"); dec = T_("dec"); A0 = T_("A0")
    Ak = [T_(f"Ak{i}") for i in range(2)]; Bk = [T_(f"Bk{i}") for i in range(2)]; PT = T_("PT")
    rv = T_("rv"); rk = T_("rk"); eg = T_("eg", (128, 1)); bk = T_("bk", (128, 1)); nwT = T_("nwT")
    Erow = T_("Erow"); qgT = T_("qgT"); glb = T_("glb", (128, 2)); kd = T_("kd", (128, 1)); kg = T_("kg")
    u_sb = [T_(f"u{i}") for i in range(2)]; o_sb = [T_(f"o{i}", (64, 128)) for i in range(2)]
    pG = P.ps([128, 128], F32, "pG"); pB = P.ps([128, 128], F32, "pB"); pK = P.ps([128, 128], F32, "pK"); pQ = P.ps([128, 128], F32, "pQ")
    pA = P.ps([128, 128], F32, "pA"); pBn = P.ps([128, 128], F32, "pBn"); pX = P.ps([128, 128], F32, "pX"); pU = P.ps([128, 128], F32, "pU")
    nct = C.CTX // 128; nlt = C.T // 128
    it = 0; iu = 0
    for d in range(2):
        for h in range(4):
            P.x("pool", "memset", [], [S[d][h]], S[d][h][:], 0.0)
        order = list(range(nct)) + [nct + i for i in range(nlt)] if d == 0 else \
            list(range(nct - 1, -1, -1)) + [nct + i for i in range(nlt - 1, -1, -1)]
        for ti in order:
            k_ = it % 2; it += 1
            r0 = ti * 128
            P.dma(QT[k_][:], C.gqkT[0][:, r0:r0 + 128].rearrange("(h p) t -> p h t", p=128), writes=[QT[k_]])
            P.dma(KT[k_][:], C.gqkT[1][:, r0:r0 + 128].rearrange("(h p) t -> p h t", p=128), writes=[KT[k_]])
            P.dma(Kt[k_][:], C.gtok[0][r0:r0 + 128, :], writes=[Kt[k_]])
            P.dma(Vt[k_][:], C.gtok[1][r0:r0 + 128, :], writes=[Vt[k_]])
            P.dma(gt[k_][:], C.pg[r0:r0 + 128, 512:528], writes=[gt[k_]])
            g_ = gt[k_]
            P.x("act", "activation", [g_], [beta], out=beta[:], in_=g_[:, 0:8], func=AF.Sigmoid)
            P.x("dve", "tensor_scalar", [beta], [nbeta], out=nbeta[:], in0=beta[:], scalar1=-1.0, scalar2=None, op0=ALU.mult)
            P.x("dve", "tensor_tensor", [g_, dtb], [ee], out=ee[:], in0=g_[:, 8:16], in1=dtb[:], op=ALU.add)
            P.x("act", "activation", [ee], [ee], out=ee[:], in_=ee[:], func=AF.Exp)
            P.x("act", "activation", [ee], [ee], out=ee[:], in_=ee[:], func=AF.Ln, bias=1.0, scale=1.0)
            P.x("dve", "tensor_tensor", [ee, negA], [la], out=la[:], in0=ee[:], in1=negA[:], op=ALU.mult)
            for h in range(4):
                j = d * 4 + h
                hs = slice(h * 128, (h + 1) * 128)
                P.x("dve", "tensor_copy", [la], [la_bc], out=la_bc[:], in_=la[:, j:j + 1].to_broadcast([128, 128]))
                P.x("dve", "tensor_copy", [beta], [be_bc], out=be_bc[:], in_=beta[:, j:j + 1].to_broadcast([128, 128]))
                P.mm(pG, pG[:], la_bc, la_bc[:], MinclT[d], MinclT[d][:], True, True)
                P.mm(pX, pX[:, 0:1], MinclT[d], MinclT[d][:], la, la[:, j:j + 1], True, True)
                P.mm(pB, pB[:], be_bc, be_bc[:], ident, ident[:], True, True)
                P.mm(pK, pK[:], KT[k_], KT[k_][:, h, :], KT[k_], KT[k_][:, h, :], True, True)
                P.mm(pQ, pQ[:], KT[k_], KT[k_][:, h, :], QT[k_], QT[k_][:, h, :], True, True)
                P.x("act", "copy", [pG], [Grow], out=Grow[:], in_=pG[:])
                P.x("dve", "tensor_copy", [pX], [gcol], out=gcol[:], in_=pX[:, 0:1])
                P.x("dve", "tensor_scalar", [gcol], [ngcol], out=ngcol[:], in0=gcol[:], scalar1=-1.0, scalar2=None, op0=ALU.mult)
                P.x("dve", "tensor_scalar", [Grow, gcol], [decT], out=decT[:], in0=Grow[:], scalar1=gcol[:], scalar2=0.0, op0=ALU.subtract, op1=ALU.min)
                P.x("act", "activation", [decT], [decT], out=decT[:], in_=decT[:], func=AF.Exp)
                P.x("dve", "tensor_tensor", [decT, MstrictT[d]], [t1], out=t1[:], in0=decT[:], in1=MstrictT[d][:], op=ALU.mult)
                P.x("dve", "tensor_tensor", [t1, pB], [t1], out=t1[:], in0=t1[:], in1=pB[:], op=ALU.mult)
                P.x("dve", "scalar_tensor_tensor", [pK, t1], [B0], out=B0[:], in0=pK[:], scalar=-1.0, in1=t1[:], op0=ALU.mult, op1=ALU.mult)
                P.x("dve", "tensor_tensor", [decT, MinclT[d]], [t1], out=t1[:], in0=decT[:], in1=MinclT[d][:], op=ALU.mult)
                P.x("dve", "tensor_tensor", [pQ, t1], [aqkT], out=aqkT[:], in0=pQ[:], in1=t1[:], op=ALU.mult)
                P.x("dve", "tensor_scalar", [Grow, gcol], [dec], out=dec[:], in0=Grow[:], scalar1=gcol[:], scalar2=0.0, op0=ALU.subtract, op1=ALU.max)
                P.x("act", "activation", [dec], [dec], out=dec[:], in_=dec[:], func=AF.Exp, scale=-1.0)
                P.x("dve", "tensor_tensor", [dec, Mstrict[d]], [dec], out=dec[:], in0=dec[:], in1=Mstrict[d][:], op=ALU.mult)
                P.x("dve", "scalar_tensor_tensor", [pK, nbeta, dec], [A0], out=A0[:], in0=pK[:], scalar=nbeta[:, j:j + 1], in1=dec[:], op0=ALU.mult, op1=ALU.mult)
                P.x("dve", "tensor_tensor", [B0, ident], [PT], out=PT[:], in0=B0[:], in1=ident[:], op=ALU.add)
                Ac, Bc = A0, B0
                for lvl in range(1, 6):
                    An = Ak[lvl % 2]; Bn = Bk[lvl % 2]
                    P.mm(pA, pA[:], Bc, Bc[:], Ac, Ac[:], True, True)
                    if lvl < 5:
                        P.mm(pBn, pBn[:], Ac, Ac[:], Bc, Bc[:], True, True)
                    P.x("act", "copy", [pA], [An], out=An[:], in_=pA[:])
                    if lvl < 5:
                        P.x("dve", "tensor_copy", [pBn], [Bn], out=Bn[:], in_=pBn[:])
                    P.mm(pX, pX[:], An, An[:], PT, PT[:], True, True)
                    P.x("dve", "tensor_tensor", [PT, pX], [PT], out=PT[:], in0=PT[:], in1=pX[:], op=ALU.add)
                    Ac, Bc = An, Bn
                P.x("dve", "tensor_scalar", [Vt[k_], beta], [rv], out=rv[:], in0=Vt[k_][:, hs], scalar1=beta[:, j:j + 1], scalar2=None, op0=ALU.mult)
                P.x("act", "activation", [gcol], [eg], out=eg[:], in_=gcol[:], func=AF.Exp)
                P.x("dve", "tensor_tensor", [eg, beta], [bk], out=bk[:], in0=eg[:], in1=beta[:, j:j + 1], op=ALU.mult)
                P.x("dve", "tensor_scalar", [Kt[k_], bk], [rk], out=rk[:], in0=Kt[k_][:, hs], scalar1=bk[:], scalar2=None, op0=ALU.mult)
                P.mm(pX, pX[:], rk, rk[:], PT, PT[:], True, True)
                P.x("act", "activation", [pX], [nwT], out=nwT[:], in_=pX[:], func=AF.Copy, scale=-1.0)
                P.x("act", "activation", [Grow], [Erow], out=Erow[:], in_=Grow[:], func=AF.Exp)
                P.x("dve", "tensor_tensor", [QT[k_], Erow], [qgT], out=qgT[:], in0=QT[k_][:, h, :], in1=Erow[:], op=ALU.mult)
                for ch in range(2):
                    lc = (63 if d == 0 else 0) + 64 * ch
                    P.x("act", "activation", [Grow], [glb], out=glb[:, ch:ch + 1], in_=Grow[:, lc:lc + 1], func=AF.Exp)
                    P.x("act", "activation", [Grow, ngcol], [kd], out=kd[64 * ch:64 * ch + 64, :], in_=Grow[64 * ch:64 * ch + 64, lc:lc + 1],
                        func=AF.Exp, bias=ngcol[64 * ch:64 * ch + 64, :], scale=1.0)
                P.x("dve", "tensor_scalar", [Kt[k_], kd], [kg], out=kg[:], in0=Kt[k_][:, hs], scalar1=kd[:], scalar2=None, op0=ALU.mult)
                Sh = S[d][h]
                for ch in ((0, 1) if d == 0 else (1, 0)):
                    cs_ = slice(64 * ch, 64 * ch + 64)
                    us = u_sb[iu % 2]; os_ = o_sb[iu % 2]; iu += 1
                    P.mm(pU, pU[:], PT, PT[:], rv, rv[:], True, False)
                    P.mm(pU, pU[:], nwT, nwT[:], Sh, Sh[:], False, True)
                    P.x("act", "copy", [pU], [us], out=us[:], in_=pU[:])
                    P.mm(pQ, pQ[0:64, :], qgT, qgT[:, cs_], Sh, Sh[:], True, False)
                    P.mm(pQ, pQ[0:64, :], aqkT, aqkT[cs_, cs_], us, us[cs_, :], False, True)
                    P.x("act", "copy", [pQ], [os_], out=os_[:], in_=pQ[0:64, :])
                    P.dma(C.go[d][r0 + 64 * ch:r0 + 64 * ch + 64, hs], os_[:], reads=[os_])
                    P.mm(pK, pK[:], kg, kg[cs_, :], us, us[cs_, :], True, True)
                    P.x("dve", "scalar_tensor_tensor", [Sh, glb, pK], [Sh], out=Sh[:], in0=Sh[:], scalar=glb[:, ch:ch + 1], in1=pK[:], op0=ALU.mult, op1=ALU.add)
    P.emit()


def phase_gdn_post(nc, C):
    P = Phase(nc, "gpo")
    ident = make_ident(P)
    gn = load_row(P, C.gdn_norm[0, :], 128, "gn")
    of = [P.sb([128, 512], F32, f"of{i}") for i in range(2)]; ob = [P.sb([128, 512], F32, f"ob{i}") for i in range(2)]
    gg = [P.sb([128, 512], F32, f"gg{i}") for i in range(2)]
    sq = P.sb([128, 512], F32, "sq"); ss = P.sb([128, 4], F32, "ss")
    bb = [P.sb([128, 512], MM, f"bb{i}") for i in range(2)]
    tp = [P.ps([128, 4, 128], MM, f"tp{i}") for i in range(2)]
    oT = [P.sb([128, 4, 128], MM, f"oT{i}") for i in range(2)]
    for ti in range(C.A // 128):
        k = ti % 2; r0 = ti * 128
        o = of[k]; o2 = ob[k]; g = gg[k]
        P.dma(o[:], C.go[0][r0:r0 + 128, :], writes=[o]); P.dma(o2[:], C.go[1][r0:r0 + 128, :], writes=[o2])
        P.dma(g[:], C.pg[r0:r0 + 128, 0:512], writes=[g])
        P.x("dve", "tensor_tensor", [o, o2], [o], out=o[:], in0=o[:], in1=o2[:], op=ALU.add)
        o3 = o[:, :].rearrange("p (h d) -> p h d", d=128); sq3 = sq[:, :].rearrange("p (h d) -> p h d", d=128)
        P.x("dve", "tensor_tensor", [o], [sq], out=sq[:], in0=o[:], in1=o[:], op=ALU.mult)
        P.x("dve", "tensor_reduce", [sq], [ss], out=ss[:], in_=sq3, axis=AX.X, op=ALU.add)
        P.x("dve", "tensor_scalar", [ss], [ss], out=ss[:], in0=ss[:], scalar1=1.0 / 128, scalar2=EPS, op0=ALU.mult, op1=ALU.add)
        P.x("act", "activation", [ss], [ss], out=ss[:], in_=ss[:], func=AF.Sqrt)
        P.x("dve", "reciprocal", [ss], [ss], out=ss[:], in_=ss[:])
        P.x("dve", "tensor_tensor", [o, ss], [o], out=o3, in0=o3, in1=ss[:, :].unsqueeze(2).to_broadcast([128, 4, 128]), op=ALU.mult)
        P.x("dve", "tensor_tensor", [o, gn], [o], out=o3, in0=o3, in1=gn[:, :].unsqueeze(1).to_broadcast([128, 4, 128]), op=ALU.mult)
        P.x("act", "activation", [g], [g], out=g[:], in_=g[:], func=AF.Silu)
        P.x("dve", "tensor_tensor", [o, g], [bb[k]], out=bb[k][:], in0=o[:], in1=g[:], op=ALU.mult)
        transpose_tile(P, oT[k], 0, bb[k], ident, tp[k], nblk=4)
        P.dma(C.mixT[512:1024, r0:r0 + 128].rearrange("(h p) t -> p h t", p=128), oT[k][:], reads=[oT[k]])
    P.emit()


def phase_na_prep(nc, C):
    P = Phase(nc, "nap")
    ident = make_ident(P)
    gains = [load_row(P, C.na_qk_gain[0, i, :], 128, f"g{i}") for i in range(2)]
    pts = [P.sb([128, 3072], F32, f"pt{i}") for i in range(2)]
    sq = P.sb([128, 1024], F32, "sq"); ss = P.sb([128, 8], F32, "ss"); qn = P.sb([128, 1024], F32, "qn")
    qb = [P.sb([128, 1024], MM, f"qb{i}") for i in range(2)]
    vb = [P.sb([128, 1024], MM, f"vb{i}") for i in range(2)]
    tp = [P.ps([128, 8, 128], MM, f"tp{i}") for i in range(2)]
    oT = [P.sb([128, 8, 128], MM, f"oT{i}") for i in range(2)]
    ib = 0
    for ti in range(C.A // 128):
        pt = pts[ti % 2]; r0 = ti * 128
        P.dma(pt[:], C.p1[r0:r0 + 128, :], writes=[pt])
        for qi in range(2):
            off = qi * 1024
            v3 = pt[:, off:off + 1024].rearrange("p (g d) -> p g d", d=128)
            sq3 = sq[:, :].rearrange("p (g d) -> p g d", d=128); qn3 = qn[:, :].rearrange("p (g d) -> p g d", d=128)
            P.x("dve", "tensor_tensor", [pt], [sq], out=sq3, in0=v3, in1=v3, op=ALU.mult)
            P.x("dve", "tensor_reduce", [sq], [ss], out=ss[:], in_=sq3, axis=AX.X, op=ALU.add)
            P.x("dve", "tensor_scalar", [ss], [ss], out=ss[:], in0=ss[:], scalar1=1.0 / 128, scalar2=EPS, op0=ALU.mult, op1=ALU.add)
            P.x("act", "activation", [ss], [ss], out=ss[:], in_=ss[:], func=AF.Sqrt)
            P.x("dve", "reciprocal", [ss], [ss], out=ss[:], in_=ss[:])
            P.x("dve", "tensor_tensor", [pt, ss], [qn], out=qn3, in0=v3, in1=ss[:, :].unsqueeze(2).to_broadcast([128, 8, 128]), op=ALU.mult)
            b = qb[ib % 2]; k = ib % 2; ib += 1
            P.x("dve", "tensor_tensor", [qn, gains[qi]], [b], out=b[:, :].rearrange("p (g d) -> p g d", d=128), in0=qn3,
                in1=gains[qi][:, :].unsqueeze(1).to_broadcast([128, 8, 128]), op=ALU.mult)
            transpose_tile(P, oT[k], 0, b, ident, tp[k], nblk=8)
            dstT = C.nqT if qi == 0 else C.nkT
            P.dma(dstT[:, r0:r0 + 128].rearrange("(h p) t -> p h t", p=128), oT[k][:], reads=[oT[k]])
        v_ = vb[ti % 2]
        P.x("act", "copy", [pt], [v_], out=v_[:], in_=pt[:, 2048:3072])
        P.dma(C.nv[r0:r0 + 128, :], v_[:], reads=[v_])
    P.emit()


def phase_na_core(nc, C):
    P = Phase(nc, "nac")
    T = C.T; CTX = C.CTX; GW = 64; rows = T // GW; WR = 8; NCK = CTX // 128
    scale = 128 ** -0.5
    ones = P.sb([128, 128], MM, "ones")
    P.x("pool", "memset", [], [ones], ones[:], 1.0)
    NE = 8 * 15 * 31
    ebc = P.sb([64, NE], F32, "ebc")
    P.dma(ebc[:], C.na_rpb[0].rearrange("h r c -> (h r c)").partition_broadcast(64), writes=[ebc])
    cmask = P.sb([64, 64], F32, "cmask")
    P.dma(cmask[:], C.na_colmaskT[:, :], writes=[cmask])
    Dj = P.sb([64, 31, 64], F32, "Dj")
    P.x("pool", "memset", [], [Dj], Dj[:], 1.0)
    for j in range(31):
        P.x("pool", "affine_select", [Dj], [Dj], out=Dj[:, j, :], in_=Dj[:, j, :], pattern=[[-1, 64]], compare_op=ALU.is_equal,
            fill=0.0, base=15 - j, channel_multiplier=1)
    EB1 = P.sb([64, 120, 64], F32, "EB1")
    for hd in range(120):
        P.x("dve", "tensor_scalar", [Dj, ebc], [EB1], out=EB1[:, hd, :], in0=Dj[:, 0, :], scalar1=ebc[:, hd * 31:hd * 31 + 1], scalar2=None, op0=ALU.mult)
        for j in range(1, 31):
            P.x("dve", "scalar_tensor_tensor", [Dj, ebc, EB1], [EB1], out=EB1[:, hd, :], in0=Dj[:, j, :], scalar=ebc[:, hd * 31 + j:hd * 31 + j + 1],
                in1=EB1[:, hd, :], op0=ALU.mult, op1=ALU.add)
    P.x("act", "activation", [EB1], [EB1], out=EB1[:], in_=EB1[:], func=AF.Exp)
    P.x("dve", "tensor_tensor", [EB1, cmask], [EB1], out=EB1[:], in0=EB1[:], in1=cmask[:, :].unsqueeze(1).to_broadcast([64, 120, 64]), op=ALU.mult)
    EB2 = P.sb([128, 8, 14, 64], F32, "EB2")
    EB14 = EB1[:, :, :].rearrange("p (h r) c -> p h r c", r=15)
    P.dma(EB2[0:64, :, :, :], EB14[:, :, 0:14, :], reads=[EB1], writes=[EB2])
    P.dma(EB2[64:128, :, :, :], EB14[:, :, 1:15, :], reads=[EB1], writes=[EB2])
    NT2 = T // 128
    qT = P.sb([128, T], MM, "qT"); kT = P.sb([128, C.A], MM, "kT")
    vE = P.sb([128, NT2, 128], MM, "vE"); vO = P.sb([128, NT2 - 1, 128], MM, "vO"); vC = P.sb([128, NCK, 128], MM, "vC")
    sps = [P.ps([128, 64], F32, f"s{i}") for i in range(4)]
    opsm = [P.ps([128, 64], F32, f"oacc{i}") for i in range(2)]
    zps = [P.ps([128, 64], F32, f"z{i}") for i in range(2)]
    pf = [P.sb([128, 64], F32, f"pf{i}") for i in range(4)]
    pb = [P.sb([128, 64], MM, f"pb{i}") for i in range(4)]
    rz = [P.sb([128, 64], F32, f"rz{i}") for i in range(2)]
    ob = [P.sb([128, 64], MM, f"ob{i}") for i in range(2)]
    isp = 0
    for h in range(8):
        hs = slice(h * 128, (h + 1) * 128)
        P.dma(qT[:], C.nqT[hs, CTX:CTX + T], writes=[qT])
        P.dma(kT[:], C.nkT[hs, :], writes=[kT])
        P.dma(vE[:], C.nv[CTX:CTX + T, hs].rearrange("(n p) d -> p n d", p=128), writes=[vE])
        P.dma(vO[:], C.nv[CTX + 64:CTX + T - 64, hs].rearrange("(n p) d -> p n d", p=128), writes=[vO])
        P.dma(vC[:], C.nv[0:CTX, hs].rearrange("(n p) d -> p n d", p=128), writes=[vC])
        for r in range(rows):
            rs = min(max(r - WR // 2, 0), rows - WR)
            kk_ = r % 2
            oa = opsm[kk_]; za = zps[kk_]
            q_ap = qT[:, r * 64:(r + 1) * 64]
            ntile = 4 + NCK
            for i in range(ntile):
                sp = sps[isp % 4]; f_ = pf[isp % 4]; b_ = pb[isp % 4]; isp += 1
                if i < 4:
                    kr0 = rs + 2 * i
                    tok0 = kr0 * 64
                    k_ap = kT[:, CTX + tok0:CTX + tok0 + 128]
                    v_ap = vE[:, tok0 // 128, :] if tok0 % 128 == 0 else vO[:, (tok0 - 64) // 128, :]
                    vt_ = vE if tok0 % 128 == 0 else vO
                else:
                    k_ap = kT[:, (i - 4) * 128:(i - 3) * 128]
                    v_ap = vC[:, i - 4, :]; vt_ = vC
                P.mm(sp, sp[:], kT, k_ap, qT, q_ap, True, True)
                if i < 4:
                    P.x("act", "activation", [sp], [f_], out=f_[:], in_=sp[:], func=AF.Exp, scale=scale)
                    dr0 = kr0 - r + 7
                    P.x("dve", "tensor_tensor", [f_, EB2], [b_], out=b_[:], in0=f_[:], in1=EB2[:, h, dr0, :], op=ALU.mult)
                else:
                    P.x("act", "activation", [sp], [b_], out=b_[:], in_=sp[:], func=AF.Exp, scale=scale)
                P.mm(oa, oa[:], vt_, v_ap, b_, b_[:], i == 0, i == ntile - 1)
                P.mm(za, za[:], ones, ones[:], b_, b_[:], i == 0, i == ntile - 1)
            P.x("dve", "reciprocal", [za], [rz[kk_]], out=rz[kk_][:], in_=za[:])
            P.x("dve", "tensor_tensor", [oa, rz[kk_]], [ob[kk_]], out=ob[kk_][:], in0=oa[:], in1=rz[kk_][:], op=ALU.mult)
            P.dma(C.mixT[hs, CTX + r * 64:CTX + (r + 1) * 64], ob[kk_][:], reads=[ob[kk_]])
    P.emit()


def phase_outproj(nc, C, l, w_out_dram, xin_of, rows):
    P = Phase(nc, f"op{l}")
    ident = make_ident(P)
    wt = P.sb([128, 8, D], MM, "w")
    load_w_bf16(P, wt, w_out_dram, D, D, splitn=1024)
    rset = sorted(set(g[0] for g in rows))
    Gm = {r: load_mod(P, C, l, 2, r, f"Gm{r}") for r in rset}
    Gf = {r: make_G(P, C, l, 4, r, C.norm_ffn[l, :], f"Gf{r}") for r in rset}
    SHf = {r: load_mod(P, C, l, 3, r, f"SHf{r}") for r in rset}
    xts = [P.sb([128, D], F32, f"x{i}") for i in range(2)]
    mts = [P.sb([128, D], F32, f"m{i}") for i in range(2)]
    mbs = [P.sb([128, D], MM, f"mb{i}") for i in range(2)]
    mT = [P.sb([128, 8, 128], MM, f"mT{i}") for i in range(2)]
    hT = [P.sb([128, 8, 128], MM, f"hT{i}") for i in range(2)]
    hbs = [P.sb([128, D], MM, f"hb{i}") for i in range(2)]
    tps = [P.ps([128, 8, 128], MM, f"tp{i}") for i in range(2)]
    acc = [P.ps([128, 512], F32, f"acc{i}") for i in range(4)]
    scr = P.sb([128, D], F32, "scr"); junk = P.sb([128, D], F32, "junk")
    sss = [P.sb([128, 1], F32, f"ss{i}") for i in range(2)]
    it = 0; ia = 0
    for (row, src, t0, nt, a0) in rows:
        for ti in range(nt // 128):
            k = it % 2; it += 1
            xt = xts[k]; mt = mts[k]; mb = mbs[k]
            r0 = a0 + ti * 128
            P.dma(xt[:], src[t0 + ti * 128:t0 + (ti + 1) * 128, :], writes=[xt])
            P.dma(mT[k][:], C.mixT[:, r0:r0 + 128].rearrange("(j p) t -> p j t", p=128), writes=[mT[k]])
            for cc in range(2):
                a = acc[ia % 4]; ia += 1
                for j in range(8):
                    P.mm(a, a[:], mT[k], mT[k][:, j, :], wt, wt[:, j, cc * 512:(cc + 1) * 512], j == 0, j == 7)
                P.x("dve", "tensor_tensor", [a, Gm[row]], [mt], out=mt[:, cc * 512:(cc + 1) * 512], in0=a[:],
                    in1=Gm[row][:, cc * 512:(cc + 1) * 512], op=ALU.mult)
            P.x("dve", "tensor_tensor", [mt, xt], [xt], out=xt[:], in0=xt[:], in1=mt[:], op=ALU.add)
            P.dma(C.xmid[r0:r0 + 128, :], xt[:], reads=[xt], writes=[C.xmid_b[r0 // 128]])
            norm_mod_tile(P, C, xt, hbs[k], Gf[row], SHf[row], scr, sss[k], junk)
            transpose_tile(P, hT[k], 0, hbs[k], ident, tps[k])
            for j in range(8):
                P.dma(C.hfT[j * 128:(j + 1) * 128, r0:r0 + 128], hT[k][:, j, :], reads=[hT[k]], writes=[C.hfT_b[r0 // 128]])
    P.emit()


def phase_ffn_in(nc, C, l, rows):
    P = Phase(nc, f"fi{l}")
    wt = P.sb([128, 8, 2 * FF], MM, "w")
    load_w_bf16(P, wt, C.ffn_w_in[l], D, 2 * FF, splitn=1408)
    hT = [P.sb([128, 8, 512], MM, f"hT{i}") for i in range(2)]
    acc = [P.ps([128, 512], F32, f"acc{i}") for i in range(4)]
    sg = [P.sb([128, 512], F32, f"sg{i}") for i in range(2)]
    ao = [P.sb([128, 512], MM, f"ao{i}") for i in range(3)]
    ia = 0; ig = 0
    for gi, (row, src, t0, nt, a0) in enumerate(rows):
        h = hT[gi % 2]
        for j in range(8):
            P.dma(h[:, j, 0:nt], C.hfT[j * 128:(j + 1) * 128, a0:a0 + nt],
                  reads=[C.hfT_b[(a0 + i * 128) // 128] for i in range(nt // 128)], writes=[h])
        for hc in range(FF // 128):
            ag = acc[ia % 4]; au = acc[(ia + 1) % 4]; ia += 2
            for j in range(8):
                P.mm(ag, ag[:, 0:nt], wt, wt[:, j, hc * 128:(hc + 1) * 128], h, h[:, j, 0:nt], j == 0, j == 7)
            for j in range(8):
                P.mm(au, au[:, 0:nt], wt, wt[:, j, FF + hc * 128:FF + (hc + 1) * 128], h, h[:, j, 0:nt], j == 0, j == 7)
            s_ = sg[ig % 2]; o = ao[ig % 3]; ig += 1
            P.x("act", "activation", [ag], [s_], out=s_[:, 0:nt], in_=ag[:, 0:nt], func=AF.Silu)
            P.x("dve", "tensor_tensor", [s_, au], [o], out=o[:, 0:nt], in0=s_[:, 0:nt], in1=au[:, 0:nt], op=ALU.mult)
            P.dma(C.aT[hc * 128:(hc + 1) * 128, a0:a0 + nt], o[:, 0:nt], reads=[o], writes=[C.aT_b[hc][gi]])
    P.emit()


def phase_ffn_out(nc, C, l, rows, dst_of):
    P = Phase(nc, f"fo{l}")
    wt = P.sb([128, FF // 128, D], MM, "w")
    load_w_bf16(P, wt, C.ffn_w_out[l], FF, D, splitn=1024)
    rset = sorted(set(g[0] for g in rows))
    Gf = {r: load_mod(P, C, l, 5, r, f"Gg{r}") for r in rset}
    aT = [P.sb([128, FF // 128, 512], MM, f"aT{i}") for i in range(2)]
    acc = [P.ps([128, 512], F32, f"acc{i}") for i in range(4)]
    xts = [P.sb([128, D], F32, f"x{i}") for i in range(2)]
    tmp = [P.sb([128, D], F32, f"tmp{i}") for i in range(2)]
    ia = 0; it = 0
    for gi, (row, src, t0, nt, a0) in enumerate(rows):
        a_ = aT[gi % 2]
        for hc in range(FF // 128):
            P.dma(a_[:, hc, 0:nt], C.aT[hc * 128:(hc + 1) * 128, a0:a0 + nt], reads=[C.aT_b[hc][gi]], writes=[a_])
        for ti in range(nt // 128):
            k = it % 2; it += 1
            xt = xts[k]; tm = tmp[k]
            r0 = a0 + ti * 128
            P.dma(xt[:], C.xmid[r0:r0 + 128, :], reads=[C.xmid_b[r0 // 128]], writes=[xt])
            for cc in range(2):
                a = acc[ia % 4]; ia += 1
                for hc in range(FF // 128):
                    P.mm(a, a[:], a_, a_[:, hc, ti * 128:(ti + 1) * 128], wt, wt[:, hc, cc * 512:(cc + 1) * 512],
                         hc == 0, hc == FF // 128 - 1)
                P.x("dve", "tensor_tensor", [a, Gf[row]], [tm], out=tm[:, cc * 512:(cc + 1) * 512], in0=a[:],
                    in1=Gf[row][:, cc * 512:(cc + 1) * 512], op=ALU.mult)
            P.x("dve", "tensor_tensor", [tm, xt], [xt], out=xt[:], in0=xt[:], in1=tm[:], op=ALU.add)
            dst, d0, db = dst_of(row, t0 + ti * 128)
            P.dma(dst[d0:d0 + 128, :], xt[:], reads=[xt], writes=[db])
    P.emit()


def build(T=8192, CTX=256, upto=99, dbg=False, da=True, gdn=True):
    nc = bass.Bass("TRN2", target_bir_lowering=False)
    C = Ctx()
    C.T = T; C.CTX = CTX; A = T + CTX; C.A = A
    def inp(name, shape):
        return nc.dram_tensor(name, list(shape), F32, kind="ExternalInput").ap()
    def scr(name, shape, dt=F32):
        return nc.dram_tensor(name, list(shape), dt, kind="Internal").ap()
    C.x = inp("x", [T, D]); C.c = inp("c", [D]); C.ctx = inp("ctx", [CTX, D]); C.c_ctx = inp("c_ctx", [D])
    C.ada_w = inp("ada_w", [2, D, 6 * D]); C.ada_b = inp("ada_b", [2, 6 * D])
    C.norm_mix = inp("norm_mix", [2, D]); C.norm_ffn = inp("norm_ffn", [2, D])
    C.ffn_w_in = inp("ffn_w_in", [2, D, 2 * FF]); C.ffn_w_out = inp("ffn_w_out", [2, FF, D])
    C.even_w_in = inp("even_w_in", [1, D, EVEN_IN]); C.even_w_out = inp("even_w_out", [1, D, D])
    C.diff_qk_gain = inp("diff_qk_gain", [1, 2, 64]); C.diff_lambda = inp("diff_lambda", [1, 4, 64])
    C.diff_subln = inp("diff_subln", [1, 128]); C.gdn_conv = inp("gdn_conv", [1, 5, 1536])
    C.gdn_a_log = inp("gdn_a_log", [1, 2, 4]); C.gdn_dt_bias = inp("gdn_dt_bias", [1, 2, 4])
    C.gdn_norm = inp("gdn_norm", [1, 128]); C.odd_w_in = inp("odd_w_in", [1, D, 3 * D])
    C.odd_w_out = inp("odd_w_out", [1, D, D]); C.na_qk_gain = inp("na_qk_gain", [1, 2, 128])
    C.na_rpb = inp("na_rpb", [1, 8, 15, 31])
    C.y = nc.dram_tensor("y", [T, D], F32, kind="ExternalOutput").ap()
    C.mod = [scr(f"mod{l}", [2, 6 * D]) for l in range(2)]
    C.mod_b = [Buf() for l in range(2)]
    C.pa = scr("pa", [A, 1536]); C.pg = scr("pg", [A, 528]); C.qkvT = scr("qkvT", [1536, A])
    C.pa_b = [Buf() for _ in range(A // 128)]; C.pg_b = [Buf() for _ in range(A // 128)]
    C.mix = inp("mix", [A, D]) if dbg else scr("mix", [A, D]); C.mix_b = [Buf() for _ in range(A // 128)]
    C.xmid = scr("xmid", [A, D]); C.xmid_b = [Buf() for _ in range(A // 128)]
    C.hfT = scr("hfT", [D, A], MM); C.hfT_b = [Buf() for _ in range(A // 128)]
    C.aT = scr("aT", [FF, A], MM)
    C.x1 = nc.dram_tensor("x1", [T, D], F32, kind="ExternalOutput").ap() if dbg else scr("x1", [T, D]); C.ctx1 = scr("ctx1", [CTX, D])
    C.x1_b = [Buf() for _ in range(T // 128)]; C.ctx1_b = [Buf() for _ in range(CTX // 128)]
    C.y_b = [Buf() for _ in range(T // 128)]
    C.neglam = scr("neglam", [128, 1]); C.neglam_b = Buf()
    C.qT = scr("qT", [512, A], MM); C.kT = scr("kT", [512, A], MM); C.vtok = scr("vtok", [A, 512], MM)
    C.qT_b = [Buf() for _ in range(A // 128)]; C.kT_b = [Buf() for _ in range(A // 128)]; C.vtok_b = [Buf() for _ in range(A // 128)]
    C.mixT = (nc.dram_tensor("mixT", [D, A], MM, kind="ExternalOutput").ap() if dbg else scr("mixT", [D, A], MM))
    C.mixT_b = [[Buf() for _ in range(A // 512 + 2)] for _ in range(8)]
    C.rope_cos = inp("rope_cos", [T, 32]); C.rope_sin = inp("rope_sin", [T, 32])
    C.gqkT = [scr(f"gqkT{i}", [512, A]) for i in range(2)]; C.gtok = [scr(f"gtok{i}", [A, 512]) for i in range(2)]
    C.go = [scr(f"go{i}", [A, 512]) for i in range(2)]
    C.p1 = scr("p1", [A, 3072]); C.nqT = scr("nqT", [D, A], MM); C.nkT = scr("nkT", [D, A], MM); C.nv = scr("nv", [A, D], MM)
    C.na_colmaskT = inp("na_colmaskT", [64, 64])
    C.groups = []
    for t0 in range(0, CTX, 512):
        C.groups.append((1, C.ctx, t0, min(512, CTX - t0), t0))
    for t0 in range(0, T, 512):
        C.groups.append((0, C.x, t0, min(512, T - t0), CTX + t0))
    C.qkvT_b = [[Buf() for _ in C.groups] for _ in range(12)]
    C.aT_b = [[Buf() for _ in C.groups] for _ in range(FF // 128)]
    phase_adaln(nc, C)
    if upto >= 1:
        phase_proj(nc, C, "pj0", 0, C.even_w_in[0], EVEN_IN, C.groups,
                   [(0, 512), (512, 1024), (1024, 1536), (3072, 3584), (3584, 3600)],
                   [(C.pa, 0, 1536), (C.pg, 1536, 2064)], 1536, 12, C.qkvT)
    if upto >= 2 and da:
        phase_da_prep(nc, C)
    if upto >= 3 and da:
        phase_da_core(nc, C)
    if upto >= 4 and gdn:
        phase_gdn_prep(nc, C)
        phase_gdn_main(nc, C)
        phase_gdn_post(nc, C)
    if upto >= 5:
        def dst0(row, t):
            return (C.ctx1, t, C.ctx1_b[t // 128]) if row == 1 else (C.x1, t, C.x1_b[t // 128])
        phase_outproj(nc, C, 0, C.even_w_out[0], None, C.groups)
        phase_ffn_in(nc, C, 0, C.groups)
        phase_ffn_out(nc, C, 0, C.groups, dst0)
    if upto >= 6:
        groups1 = [(row, (C.ctx1 if row == 1 else C.x1), t0, nt, a0) for (row, src, t0, nt, a0) in C.groups]
        lat1 = [g for g in groups1 if g[0] == 0]
        phase_proj(nc, C, "pj1", 1, C.odd_w_in[0], 3 * D, groups1, [(i * 512, (i + 1) * 512) for i in range(6)],
                   [(C.p1, 0, 3072)], 0, 0, None)
        phase_na_prep(nc, C)
        phase_na_core(nc, C)
    if upto >= 7:
        phase_outproj(nc, C, 1, C.odd_w_out[0], None, lat1)
        phase_ffn_in(nc, C, 1, lat1)
        phase_ffn_out(nc, C, 1, lat1, lambda row, t: (C.y, t, C.y_b[t // 128]))
    if getattr(nc, '_semstate', None) is not None:
        nc._semstate['es'].close()
    return nc, C


_CONST_NAMES = ["rope_cos", "rope_sin", "na_colmaskT"]
_IN_NAMES = ["x", "c", "ctx", "c_ctx", "ada_w", "ada_b", "norm_mix", "norm_ffn", "ffn_w_in", "ffn_w_out",
             "even_w_in", "even_w_out", "diff_qk_gain", "diff_lambda", "diff_subln", "gdn_conv", "gdn_a_log",
             "gdn_dt_bias", "gdn_norm", "odd_w_in", "odd_w_out", "na_qk_gain", "na_rpb"]


def consts(T):
    n_freq = 16
    inv = (np.float32(10000.0) ** (-np.arange(n_freq, dtype=np.float32) / np.float32(n_freq))).astype(np.float32)
    t = np.arange(T)
    row = (t // 64).astype(np.float32); col = (t % 64).astype(np.float32)
    ang = np.concatenate([row[:, None] * inv, col[:, None] * inv], axis=-1).astype(np.float32)
    cols = np.arange(64)
    c_start = np.clip(cols - 8, 0, 48)
    col_mask = (cols[None, :] >= c_start[:, None]) & (cols[None, :] < c_start[:, None] + 16)
    return {"rope_cos": np.cos(ang).astype(np.float32), "rope_sin": np.sin(ang).astype(np.float32),
            "na_colmaskT": np.ascontiguousarray(col_mask.T).astype(np.float32)}


def kernel(**inputs):
    B = inputs["x"].shape[0]
    nc, C = build()
    cst = consts(8192)
    in_maps = []
    for b in range(B):
        m = {}
        for k in _IN_NAMES:
            v = np.ascontiguousarray(inputs[k], dtype=np.float32)
            if k in ("x", "c", "ctx"):
                v = np.ascontiguousarray(v[b])
            m[k] = v
        m.update(cst)
        in_maps.append(m)
    res = run_bass_kernel_spmd(nc, in_maps, core_ids=list(range(B)))
    return np.stack([np.asarray(r["y"], dtype=np.float32) for r in res.results], axis=0)
```
